# Optimizing a Trainium2 kernel written in Bass

```python
import math
import jax, jax.numpy as jnp
from jax import lax
import numpy as np

D_MODEL = 1024
BATCH = 8
SEQ = 2048
DEPTH = 2

HEAD_DIM = 64
BLK = 128
H_SB = 4
DIL_PATTERNS = ((128, 1), (512, 4), (2048, 16))
H_PER_DIL = 2
H_DIL = H_PER_DIL * len(DIL_PATTERNS)
H_SWA_Q = 6
H_SWA_KV = 2
SWA_WINDOW = 128
N_BUCKETS = 32
MAX_REL_DIST = 2048
N_SOFT_HEADS = H_DIL + H_SWA_Q
D_FF = 2816
RMS_EPS = 1e-6
ADA_CHUNKS = 9
IN_SPLITS = (H_SB * HEAD_DIM, H_SB * HEAD_DIM, H_SB * HEAD_DIM,
             H_DIL * HEAD_DIM, H_DIL * HEAD_DIM, H_DIL * HEAD_DIM,
             H_SWA_Q * HEAD_DIM, H_SWA_KV * HEAD_DIM, H_SWA_KV * HEAD_DIM,
             D_MODEL, D_MODEL, D_MODEL)
D_IN = sum(IN_SPLITS)

kernel_name = 'hybrid_sb_dilated_swa_macaron'


def rmsnorm(x, g):
    xf = x.astype(jnp.float32)
    y = xf * lax.rsqrt(jnp.mean(xf * xf, axis=-1, keepdims=True) + RMS_EPS)
    return (y * g.astype(jnp.float32)).astype(x.dtype)


def swiglu(h, wg, wu, wd):
    return (jax.nn.silu(h @ wg) * (h @ wu)) @ wd


def t5_bucket(n):
    max_exact = N_BUCKETS // 2
    nf = jnp.maximum(n, 1).astype(jnp.float32)
    large = max_exact + (jnp.log(nf / max_exact) / math.log(MAX_REL_DIST / max_exact)
                         * (N_BUCKETS - max_exact)).astype(jnp.int32)
    large = jnp.minimum(large, N_BUCKETS - 1)
    return jnp.where(n < max_exact, n, large)


def band_bias(table_cols, dilation):
    rel = jnp.arange(BLK)[:, None] + BLK - jnp.arange(2 * BLK)[None, :]
    b = t5_bucket(jnp.maximum(rel, 0) * dilation)
    return jnp.transpose(table_cols[b], (2, 0, 1)).astype(jnp.float32)


def banded_attention(q, k, v, bias, max_dist, sinks=None):
    N, L, Hk, G, Dh = q.shape
    nb = L // BLK
    qb = q.reshape(N, nb, BLK, Hk, G, Dh).astype(jnp.float32)

    def band(t):
        tb = t.reshape(N, nb, BLK, Hk, Dh).astype(jnp.float32)
        prev = jnp.pad(tb, ((0, 0), (1, 0), (0, 0), (0, 0), (0, 0)))[:, :-1]
        return jnp.concatenate([prev, tb], axis=2)

    kk, vv = band(k), band(v)
    s = jnp.einsum('nbqhgd,nbkhd->nbhgqk', qb, kk) * (Dh ** -0.5) + bias
    rel = jnp.arange(BLK)[:, None] + BLK - jnp.arange(2 * BLK)[None, :]
    in_band = (rel >= 0) & (rel <= max_dist)
    key_pos = jnp.arange(nb)[:, None] * BLK - BLK + jnp.arange(2 * BLK)[None, :]
    mask = in_band[None] & (key_pos >= 0)[:, None, :]
    s = jnp.where(mask[None, :, None, None], s, -jnp.inf)
    m = jnp.max(s, axis=-1)
    if sinks is not None:
        sk = sinks.astype(jnp.float32)[None, None, :, :, None]
        m = jnp.maximum(m, sk)
    p = jnp.exp(s - m[..., None])
    denom = jnp.sum(p, axis=-1)
    if sinks is not None:
        denom = denom + jnp.exp(sk - m)
    o = jnp.einsum('nbhgqk,nbkhd->nbqhgd', p, vv) / jnp.transpose(denom, (0, 1, 4, 2, 3))[..., None]
    lse = jnp.transpose(m + jnp.log(denom), (0, 1, 4, 2, 3))
    return o.reshape(N, L, Hk, G, Dh), lse.reshape(N, L, Hk, G)


def stick_breaking_mixer(q, k, v):
    Bn, S, H, Dh = q.shape
    nb = S // BLK
    kf, vf = k.astype(jnp.float32), v.astype(jnp.float32)
    qb = q.astype(jnp.float32).reshape(Bn, nb, BLK, H, Dh).transpose(1, 0, 2, 3, 4)
    key_pos = jnp.arange(S)

    def block(args):
        qblk, i = args
        z = jnp.einsum('bqhd,bkhd->bhqk', qblk, kf) * (Dh ** -0.5)
        q_pos = i * BLK + jnp.arange(BLK)
        before = key_pos[None, :] < q_pos[:, None]
        log_fail = jnp.where(before, jax.nn.log_sigmoid(-z), 0.0)
        between = lax.cumsum(log_fail, axis=3, reverse=True) - log_fail
        w = jnp.where(before, jnp.exp(jax.nn.log_sigmoid(z) + between), 0.0)
        return jnp.einsum('bhqk,bkhd->bqhd', w, vf)

    o = lax.map(block, (qb, jnp.arange(nb)))
    return o.transpose(1, 0, 2, 3, 4).reshape(Bn, S, H, Dh)


def dilated_mixer(q, k, v, rel_table):
    Bn, S = q.shape[:2]
    outs, lses = [], []
    for g, (w, d) in enumerate(DIL_PATTERNS):
        hs = slice(g * H_PER_DIL, (g + 1) * H_PER_DIL)
        Ls = S // d
        Lp = -(-Ls // BLK) * BLK

        def sub(t):
            t = t[:, :, hs].reshape(Bn, Ls, d, H_PER_DIL, HEAD_DIM).transpose(0, 2, 1, 3, 4)
            t = t.reshape(Bn * d, Ls, H_PER_DIL, HEAD_DIM)
            return jnp.pad(t, ((0, 0), (0, Lp - Ls), (0, 0), (0, 0)))

        bias = band_bias(rel_table[:, hs], d)[:, None]
        o, lse = banded_attention(sub(q)[:, :, :, None], sub(k), sub(v), bias, w // d)
        o = o[:, :Ls, :, 0].reshape(Bn, d, Ls, H_PER_DIL, HEAD_DIM).transpose(0, 2, 1, 3, 4)
        lse = lse[:, :Ls, :, 0].reshape(Bn, d, Ls, H_PER_DIL).transpose(0, 2, 1, 3)
        outs.append(o.reshape(Bn, S, H_PER_DIL, HEAD_DIM))
        lses.append(lse.reshape(Bn, S, H_PER_DIL))
    alpha = jax.nn.softmax(jnp.stack(lses), axis=0)
    return jnp.sum(alpha[..., None] * jnp.stack(outs), axis=0)


def swa_mixer(q, k, v, rel_table, sinks):
    Bn, S = q.shape[:2]
    G = H_SWA_Q // H_SWA_KV
    bias = band_bias(rel_table[:, H_DIL:], 1).reshape(H_SWA_KV, G, BLK, 2 * BLK)
    o, _ = banded_attention(q.reshape(Bn, S, H_SWA_KV, G, HEAD_DIM), k, v, bias,
                            SWA_WINDOW - 1, sinks.reshape(H_SWA_KV, G))
    return o.reshape(Bn, S, H_SWA_Q, HEAD_DIM)


def setup_inputs(seed: int = 0) -> dict:
    key = jax.random.key(seed)
    ks = jax.random.split(key, 18)
    f32 = jnp.float32
    nrm = lambda k, shape, scale: jax.random.normal(k, shape, f32) * scale
    return {
        'x': nrm(ks[0], (BATCH, SEQ, D_MODEL), 1.0),
        'c': nrm(ks[1], (BATCH, D_MODEL), 1.0),
        'w_ada': nrm(ks[2], (DEPTH, D_MODEL, ADA_CHUNKS * D_MODEL), 0.5 * D_MODEL ** -0.5),
        'b_ada': nrm(ks[3], (DEPTH, ADA_CHUNKS * D_MODEL), 0.02),
        'norm_gain': 1.0 + nrm(ks[4], (DEPTH, 3, D_MODEL), 0.05),
        'w_ffn_gate': nrm(ks[5], (DEPTH, 2, D_MODEL, D_FF), D_MODEL ** -0.5),
        'w_ffn_up': nrm(ks[6], (DEPTH, 2, D_MODEL, D_FF), D_MODEL ** -0.5),
        'w_ffn_down': nrm(ks[7], (DEPTH, 2, D_FF, D_MODEL), D_FF ** -0.5),
        'w_in': nrm(ks[8], (DEPTH, D_MODEL, D_IN), D_MODEL ** -0.5),
        'w_br_sb': nrm(ks[9], (DEPTH, H_SB * HEAD_DIM, D_MODEL), (H_SB * HEAD_DIM) ** -0.5),
        'w_br_dil': nrm(ks[10], (DEPTH, H_PER_DIL * HEAD_DIM, D_MODEL), (H_PER_DIL * HEAD_DIM) ** -0.5),
        'w_br_swa': nrm(ks[11], (DEPTH, H_SWA_Q * HEAD_DIM, D_MODEL), (H_SWA_Q * HEAD_DIM) ** -0.5),
        'w_out': nrm(ks[12], (DEPTH, D_MODEL, D_MODEL), D_MODEL ** -0.5),
        'sinks': nrm(ks[13], (DEPTH, H_SWA_Q), 0.5),
        'rel_bias': nrm(ks[14], (N_BUCKETS, N_SOFT_HEADS), 0.5),
        'final_gain': 1.0 + nrm(ks[15], (D_MODEL,), 0.05),
    }


def reference(x, c, w_ada, b_ada, norm_gain, w_ffn_gate, w_ffn_up, w_ffn_down, w_in,
              w_br_sb, w_br_dil, w_br_swa, w_out, sinks, rel_bias, final_gain):
    Bn, S, _ = x.shape
    split_idx = np.cumsum(IN_SPLITS)[:-1].tolist()
    heads = lambda t: t.reshape(Bn, S, -1, HEAD_DIM)
    for l in range(DEPTH):
        mod = (jax.nn.silu(c) @ w_ada[l] + b_ada[l]).reshape(Bn, 3, 3, D_MODEL)[:, :, :, None, :]

        def pre(xx, j):
            return rmsnorm(xx, norm_gain[l, j]) * (1 + mod[:, j, 1]) + mod[:, j, 0]

        h = pre(x, 0)
        x = x + 0.5 * mod[:, 0, 2] * swiglu(h, w_ffn_gate[l, 0], w_ffn_up[l, 0], w_ffn_down[l, 0])

        h = pre(x, 1)
        (q_sb, k_sb, v_sb, q_dil, k_dil, v_dil, q_swa, k_swa, v_swa,
         g_sb, g_dil, g_swa) = jnp.split(h @ w_in[l], split_idx, axis=-1)
        o_sb = stick_breaking_mixer(heads(q_sb), heads(k_sb), heads(v_sb)).reshape(Bn, S, -1).astype(x.dtype)
        o_dil = dilated_mixer(heads(q_dil), heads(k_dil), heads(v_dil), rel_bias).reshape(Bn, S, -1).astype(x.dtype)
        o_swa = swa_mixer(heads(q_swa), heads(k_swa), heads(v_swa), rel_bias, sinks[l]).reshape(Bn, S, -1).astype(x.dtype)
        merged = (jax.nn.sigmoid(g_sb) * (o_sb @ w_br_sb[l])
                  + jax.nn.sigmoid(g_dil) * (o_dil @ w_br_dil[l])
                  + jax.nn.sigmoid(g_swa) * (o_swa @ w_br_swa[l]))
        x = x + mod[:, 1, 2] * (merged @ w_out[l])

        h = pre(x, 2)
        x = x + 0.5 * mod[:, 2, 2] * swiglu(h, w_ffn_gate[l, 1], w_ffn_up[l, 1], w_ffn_down[l, 1])
    return rmsnorm(x, final_gain)
```

```python
import os
import numpy as np
import concourse.bass as bass
import concourse.mybir as mybir
from concourse.bass_utils import run_bass_kernel_spmd

F32 = mybir.dt.float32
BF16 = mybir.dt.bfloat16
AF = mybir.ActivationFunctionType
ALU = mybir.AluOpType

D = 1024
S = 2048
DFF = 2816
NJ = 22
KC = 8
NT = 4
TS = 512
NB = 16
L = 2
EPS = 1e-6
BIG = 30000.0
SAME_ENGINE_SYNC = True
FFN_GROUPS = [(0, 6), (6, 12), (12, 17), (17, 22)]

ENGS = ["pe", "act", "dve", "pool", "sp"]


class Prog:
    def __init__(self):
        self.ops = []
        self.tw = {}
        self.tr = {}
        self.barrier_pending = {e: set() for e in ENGS}
        self.since_barrier = []
        self.last_of = {e: None for e in ENGS}
        self.final_dma = []

    def op(self, eng, fn, r=(), w=(), wj=(), dma_key=None):
        oid = len(self.ops)
        deps = set()
        for t in r:
            deps.update(self.tw.get(t, ()))
        for t in w:
            deps.update(self.tr.get(t, ()))
            deps.update(self.tw.get(t, ()))
        for t in wj:
            deps.update(self.tr.get(t, ()))
        if self.barrier_pending[eng]:
            deps.update(self.barrier_pending[eng])
            self.barrier_pending[eng] = set()
        for t in r:
            self.tr.setdefault(t, []).append(oid)
        for t in w:
            self.tw[t] = [oid]
            self.tr[t] = []
        for t in wj:
            if self.tr.get(t):
                self.tw[t] = [oid]
                self.tr[t] = []
            else:
                self.tw.setdefault(t, []).append(oid)
        deps.discard(oid)
        self.ops.append(dict(eng=eng, fn=fn, deps=deps, dma_key=dma_key))
        self.last_of[eng] = oid
        if dma_key is not None:
            self.since_barrier.append(oid)
        return oid

    def barrier(self):
        pend = set(self.since_barrier)
        for e in ENGS:
            if self.last_of[e] is not None:
                pend.add(self.last_of[e])
        for e in ENGS:
            self.barrier_pending[e] = set(pend)
        self.since_barrier = []

    def emit(self, nc, block, sems):
        ops = self.ops
        by_eng = {e: [] for e in ENGS}
        for oid, o in enumerate(ops):
            by_eng[o["eng"]].append(oid)
        needed = set()
        for o in ops:
            needed.update(o["deps"])
        sigval = {}
        for e in ENGS:
            cnt = 0
            for oid in by_eng[e]:
                if ops[oid]["dma_key"] is None and oid in needed:
                    cnt += 1
                    sigval[oid] = cnt
        dma_cnt = {}
        dmaval = {}
        for oid, o in enumerate(ops):
            k = o["dma_key"]
            if k is not None:
                dma_cnt[k] = dma_cnt.get(k, 0) + 16
                dmaval[oid] = dma_cnt[k]
        eng_sem = {e: sems.get(("eng", e)) for e in ENGS}

        def run_engine(e, eobj):
            seen = {}
            for oid in by_eng[e]:
                o = ops[oid]
                waits = {}
                for d in o["deps"]:
                    od = ops[d]
                    if od["dma_key"] is not None:
                        key = ("dma", od["dma_key"])
                        val = dmaval[d]
                    else:
                        if od["eng"] == e and (e == "pe" or not SAME_ENGINE_SYNC):
                            continue
                        key = ("eng", od["eng"])
                        val = sigval[d]
                    if seen.get(key, 0) >= val:
                        continue
                    if waits.get(key, 0) < val:
                        waits[key] = val
                for key, val in waits.items():
                    eobj.wait_ge(sems.get(key), val)
                    seen[key] = val
                inst = o["fn"](eobj)
                if o["dma_key"] is not None:
                    inst.then_inc(sems.get(("dma", o["dma_key"])), 16)
                elif oid in sigval:
                    inst.then_inc(eng_sem[e], 1)
            if e == "sp":
                for k in self.final_dma:
                    eobj.wait_ge(sems.get(("dma", k)), dma_cnt[k])

        @block.tensor
        def _(t):
            run_engine("pe", t)

        @block.scalar
        def _(a):
            run_engine("act", a)

        @block.vector
        def _(v):
            run_engine("dve", v)

        @block.gpsimd
        def _(g):
            run_engine("pool", g)

        @block.sync
        def _(s):
            run_engine("sp", s)


class SemPool:
    def __init__(self, nc, stack):
        self.nc = nc
        self.stack = stack
        self.d = {}

    def get(self, key):
        if key not in self.d:
            name = "s_" + "_".join(str(x) for x in (key if isinstance(key, tuple) else (key,)))
            name = name.replace("(", "").replace(")", "").replace(",", "_").replace(" ", "").replace("'", "")
            self.d[key] = self.stack.enter_context(self.nc.semaphore(name))
        return self.d[key]


class Scratch:
    def __init__(self, base):
        self.base = base
        self.off = 0
        self.W = base.shape[1]

    def alloc(self, shape, dtype):
        n = int(np.prod(shape))
        esz = 4 if dtype == F32 else 2
        words = (n * esz + 3) // 4
        words = (words + 7) // 8 * 8
        assert self.off + words <= self.W, ("scratch overflow", self.off, words, self.W)
        ap = self.base[:, self.off:self.off + words]
        self.off += words
        if dtype != F32:
            ap = ap.bitcast(dtype)
        ap = ap[:, 0:n]
        if len(shape) == 2:
            return ap.rearrange("p (a b) -> p a b", a=shape[0])
        if len(shape) == 3:
            return ap.rearrange("p (a b c) -> p a b c", a=shape[0], b=shape[1])
        if len(shape) == 4:
            return ap.rearrange("p (a b c d) -> p a b c d", a=shape[0], b=shape[1], c=shape[2])
        return ap

    def mark(self):
        return self.off

    def release(self, m):
        self.off = m


class Builder:
    def __init__(self, n_layers=L, stop=None):
        self.n_layers = n_layers
        self.stop = stop
        self.P = Prog()
        self.dbg_outs = {}

    def mm(self, out, lhsT, rhs, start, stop, r, w=(), wj=(), **kw):
        self.P.op("pe", lambda e: e.matmul(out, lhsT=lhsT, rhs=rhs, start=start, stop=stop,
                                           skip_group_check=True, **kw), r=r, w=w, wj=wj)

    def act(self, out, in_, func, r, w=(), wj=(), bias=None, scale=None):
        kw = {}
        if bias is not None:
            kw["bias"] = bias
        if scale is not None:
            kw["scale"] = scale
        self.P.op("act", lambda e: e.activation(out=out, in_=in_, func=func, **kw), r=r, w=w, wj=wj)

    def tt(self, eng, out, in0, in1, op, r, w=(), wj=()):
        self.P.op(eng, lambda e: e.tensor_tensor(out=out, in0=in0, in1=in1, op=op), r=r, w=w, wj=wj)

    def ts(self, eng, out, in0, s1, s2, op0, op1, r, w=(), wj=()):
        if s2 is None:
            self.P.op(eng, lambda e: e.tensor_scalar(out=out, in0=in0, scalar1=s1, scalar2=None, op0=op0),
                      r=r, w=w, wj=wj)
        else:
            self.P.op(eng, lambda e: e.tensor_scalar(out=out, in0=in0, scalar1=s1, scalar2=s2, op0=op0, op1=op1),
                      r=r, w=w, wj=wj)

    def stt(self, out, in0, scalar, in1, op0, op1, r, w=(), wj=()):
        self.P.op("dve", lambda e: e.scalar_tensor_tensor(out=out, in0=in0, scalar=scalar, in1=in1,
                                                          op0=op0, op1=op1), r=r, w=w, wj=wj)

    def copy(self, eng, out, in_, r, w=(), wj=()):
        if eng == "act":
            self.P.op("act", lambda e: e.copy(out=out, in_=in_), r=r, w=w, wj=wj)
        else:
            self.P.op(eng, lambda e: e.tensor_copy(out=out, in_=in_), r=r, w=w, wj=wj)

    def recip(self, out, in_, r, w=(), wj=()):
        self.P.op("dve", lambda e: e.reciprocal(out=out, in_=in_), r=r, w=w, wj=wj)

    def memset(self, eng, ap, val, w=(), wj=()):
        self.P.op(eng, lambda e: e.memset(ap, val), w=w, wj=wj)

    def dma(self, eng, out, in_, key, r=(), w=(), wj=(), **kw):
        self.P.op(eng, lambda e: e.dma_start(out=out, in_=in_, **kw), r=r, w=w, wj=wj, dma_key=key)

    def ring_push(self, ring, src, view):
        st = self.rings[ring]
        slot = st["n"] % len(st["slots"])
        st["n"] += 1
        live = st.setdefault("live", set())
        assert slot not in live, ("ring slot still live", ring, slot)
        live.add(slot)
        dst = view(st["slots"][slot])
        tok = (ring, slot)
        self.dma("pool", dst, src, key=(ring, slot), w=[tok], max_dma_last_dim=4096)
        return dst, tok

    def ring_free(self, tok):
        self.rings[tok[0]]["live"].discard(tok[1])

    def build(self):
        from contextlib import ExitStack
        nc = bass.Bass("TRN2", target_bir_lowering=False)
        self.nc = nc
        nl = self.n_layers
        din = lambda name, shape: nc.dram_tensor(name, list(shape), F32, kind="ExternalInput").ap()
        self.xT_d = din("xT", [D, S])
        self.cc_d = din("c_c", [128, 8])
        self.wada_d = din("wada_t", [L, 72, 128, 8, 128])
        self.bada_d = din("bada_c", [128, L, 72])
        self.ng_d = din("ng_c", [128, L, 3, 8])
        self.fg_d = din("fg_c", [128, 8])
        self.wgu_d = din("wgu", [L, 2, NJ, 128, 2, 8, 128])
        self.wd_d = din("wd", [L, 2, NJ, 128, D])
        self.win_d = din("win_t", [L, 45, 128, 8, 128])
        self.wv_d = din("wv", [L, 128, 8, 768])
        self.wbr_d = din("wbr", [L, 6, 128, D])
        self.wout_d = din("wout_t", [L, 8, 128, 8, 128])
        self.sinks_d = din("sinks_b", [128, L, 6])
        self.biasT_d = din("biasT", [128, 12, 256])
        self.cmask_d = din("cmask", [128, 2, 256])
        self.cst_d = din("cst", [128, 4, 128])
        self.mb_d = din("mbias", [128, 4, 512])
        self.outT_d = nc.dram_tensor("outT", [D, S], F32, kind="ExternalOutput").ap()

        with ExitStack() as st:
            self.sems = SemPool(nc, st)
            sb = lambda name, shape, dt: st.enter_context(nc.sbuf_tensor(name, list(shape), dt))
            self.xT = sb("xTs", [128, 8, S], F32)
            self.hT = sb("hTs", [128, 8, S], BF16)
            ringA = [sb(f"ringA{i}", [128, 2048], BF16) for i in range(4)]
            ringD = [sb(f"ringD{i}", [128, 1024], BF16) for i in range(6)]
            ringM = [sb(f"ringM{i}", [128, 1024], BF16) for i in range(2)]
            self.rings = {"rA": dict(slots=ringA, n=0), "rD": dict(slots=ringD, n=0), "rM": dict(slots=ringM, n=0)}
            self.cb16 = sb("cb16", [128, 4, 128], BF16)
            self.onesf = sb("onesf", [128, 64], F32)
            self.MB = sb("MB", [128, 4, 512], BF16)
            self.cc = sb("cc", [128, 8], F32)
            self.sc = sb("sc", [128, 8], BF16)
            self.bada = sb("bada", [128, L, 72], F32)
            self.ng = sb("ng", [128, L, 3, 8], F32)
            self.fg = sb("fg", [128, 8], F32)
            self.sinks = sb("sinks", [128, L, 6], F32)
            self.esink = sb("esink", [128, L, 6], F32)
            self.mod = sb("modsb", [128, L, 72], F32)
            self.acol = sb("acol", [128, L, 3, 8], F32)
            self.gcol = sb("gcol", [128, L, 3, 8], F32)
            scr_base = sb("scr", [128, 17408], F32)
            self.scr = Scratch(scr_base)
            self.ps = [st.enter_context(nc.psum_tensor(f"ps{i}", [128, 512], F32)) for i in range(8)]
            for name, (shape, dt) in self.dbg_spec().items():
                self.dbg_outs[name] = nc.dram_tensor(name, list(shape), dt, kind="ExternalOutput").ap()

            self.program()

            for e in ENGS:
                self.sems.get(("eng", e))
            for o in self.P.ops:
                if o["dma_key"] is not None:
                    self.sems.get(("dma", o["dma_key"]))
            with nc.Block() as block:
                self.P.emit(nc, block, self.sems)
        return nc

    def dbg_spec(self):
        return {}

    def program(self):
        P = self.P
        nl = self.n_layers
        self.load_consts()
        self.mod_pieces = {}
        for l in range(nl):
            self.mod_pieces[l] = list(range(72))
        self.emit_mod(0, 72)
        for l in range(nl):
            self.cur_bg = l + 1 if l + 1 < nl else None
            self.ffn(l, 0)
            if self.stop == "__done__":
                return
            if self.stop == f"x{l}a":
                return self.dump_x()
            self.mixer(l)
            if self.stop == "__done__":
                return
            if self.stop == f"x{l}m":
                return self.dump_x()
            self.ffn(l, 1)
            if self.stop == f"x{l}b":
                return self.dump_x()
            if self.cur_bg is not None:
                self.emit_mod(self.cur_bg, 72)
        self.final_norm()

    def dump_h(self):
        for c in range(8):
            for t in range(NT):
                tsl = slice(t * TS, (t + 1) * TS)
                self.copy("dve", self.xT[:, c, tsl], self.hT[:, c, tsl], r=[("h", c, t)], w=[("x", c, t)])
        self.stop = "__done__"
        return self.dump_x()

    def dump_small(self, ap, n, toks):
        self.copy("dve", self.xT[:, 0, 0:n], ap, r=toks, w=[("x", 0, t) for t in range(NT)])
        self.stop = "__done__"
        return self.dump_x()

    def dump_x(self):
        for c in range(8):
            self.dma("sp", self.outT_d[c * 128:(c + 1) * 128, :], self.xT[:, c, :], key="out",
                     r=[("x", c, t) for t in range(NT)])
        self.P.final_dma.append("out")

    def bg_step(self, n=1):
        if self.cur_bg is not None:
            self.emit_mod(self.cur_bg, n)

    def load_consts(self):
        for c in range(8):
            self.dma("sp", self.xT[:, c, :], self.xT_d[c * 128:(c + 1) * 128, :], key=("xin", c),
                     w=[("x", c, t) for t in range(NT)])
        self.dma("pool", self.cb16[:], self.cst_d[:, :, :], key="cst0", w=["cb16"], max_dma_last_dim=2048)
        for dk_ in range(4):
            self.dma("pool", self.MB[:, dk_, :], self.mb_d[:, dk_, :], key=("cst1", dk_), wj=["MB"],
                     max_dma_last_dim=2048)
        self.dma("sp", self.cc[:], self.cc_d[:, :], key="cst2", w=["cc"])
        self.dma("sp", self.bada[:], self.bada_d[:, :, :], key="cst3", w=["bada"])
        self.dma("sp", self.ng[:], self.ng_d[:, :, :, :], key="cst4", w=["ng"])
        self.dma("sp", self.fg[:], self.fg_d[:, :], key="cst5", w=["fg"])
        self.dma("sp", self.sinks[:], self.sinks_d[:, :, :], key="cst6", w=["sinks"])
        self.memset("dve", self.onesf[:], 1.0, w=["onesf"])
        self.act(self.sc[:], self.cc[:], AF.Silu, r=["cc"], w=["sc"])
        self.act(self.esink[:], self.sinks[:], AF.Exp, r=["sinks"], w=["esink"])

    def emit_mod(self, l, n):
        psM = self.ps[7]
        for _ in range(n):
            if not self.mod_pieces[l]:
                return
            j = self.mod_pieces[l].pop(0)
            buf, tok = self.ring_push("rM", self.wada_d[l, j],
                                      lambda s: s[:, :].rearrange("p (k n) -> p k n", k=8))
            for k in range(8):
                self.mm(psM[:, j:j + 1], buf[:, k, :], self.sc[:, k:k + 1],
                        start=(k == 0), stop=(k == 7), r=[tok, "sc"],
                        w=[("ps", 7)] if k == 0 else (), wj=[("ps", 7)] if k > 0 else ())
            self.ring_free(tok)
            if j % 24 == 23:
                s = j // 24
                cols = slice(s * 24, (s + 1) * 24)
                self.tt("dve", self.mod[:, l, cols], psM[:, cols], self.bada[:, l, cols], ALU.add,
                        r=[("ps", 7), "bada"], w=[("mod", l, s)])
                self.stt(self.acol[:, l, s, :], self.mod[:, l, s * 24 + 8:s * 24 + 16], 1.0, self.ng[:, l, s, :],
                         ALU.add, ALU.mult, r=[("mod", l, s), "ng"], w=[("acol", l, s)])
                gscale = 1.0 if s == 1 else 0.5
                self.ts("dve", self.gcol[:, l, s, :], self.mod[:, l, s * 24 + 16:s * 24 + 24], gscale, None,
                        ALU.mult, None, r=[("mod", l, s)], w=[("gcol", l, s)])

    def norm(self, l, s):
        m = self.scr.mark()
        sq = self.scr.alloc([3, 512], BF16)
        sd = self.scr.alloc([2, 512], F32)
        rstd = self.scr.alloc([2, 512], F32)
        tmpn = self.scr.alloc([2, 512], F32)
        ONES = self.cb16[:, 0, :]
        psN = self.ps[6]
        n = 0
        for t in range(NT):
            tsl = slice(t * TS, (t + 1) * TS)
            for c in range(8):
                self.act(sq[:, c % 3, :], self.xT[:, c, tsl], AF.Square, r=[("x", c, t)], w=[("sq", c % 3)])
                self.mm(psN[:, :], ONES, sq[:, c % 3, :], start=(c == 0), stop=(c == 7),
                        r=[("sq", c % 3), "cb16"], w=[("ps", 6)] if c == 0 else (), wj=[("ps", 6)] if c > 0 else ())
            self.act(sd[:, t % 2, :], psN[:, :], AF.Sqrt, r=[("ps", 6)], w=[("sd", t % 2)], bias=EPS,
                     scale=1.0 / D)
            self.recip(rstd[:, t % 2, :], sd[:, t % 2, :], r=[("sd", t % 2)], w=[("rstd", t % 2)])
            for c in range(8):
                b = n % 2
                n += 1
                self.stt(tmpn[:, b, :], self.xT[:, c, tsl], self.acol[:, l, s, c:c + 1], rstd[:, t % 2, :],
                         ALU.mult, ALU.mult, r=[("x", c, t), ("acol", l, s), ("rstd", t % 2)], w=[("tmpn", b)])
                self.act(self.hT[:, c, tsl], tmpn[:, b, :], AF.Identity, r=[("tmpn", b), ("mod", l, s)],
                         w=[("h", c, t)], bias=self.mod[:, l, s * 24 + c:s * 24 + c + 1])
        self.scr.release(m)

    def ffn(self, l, f):
        s = 0 if f == 0 else 2
        self.norm(l, s)
        if self.stop == f"h{l}{'ab'[f]}":
            return self.dump_h()
        if self.stop == f"mod{l}":
            return self.dump_small(self.mod[:, l, :], 72, [("mod", l, s_) for s_ in range(3)])
        m = self.scr.mark()
        AT = self.scr.alloc([6, S], BF16)
        sg = self.scr.alloc([2, 512], F32)
        nsg = 0
        nd = 0
        for (j0, j1) in FFN_GROUPS:
            wd = []
            for j in range(j0, j1):
                wd.append(self.ring_push("rD", self.wd_d[l, f, j], lambda s_: s_[:, :]))
            pend = []
            for j in range(j0, min(j0 + 2, j1)):
                pend.append(self.ring_push("rA", self.wgu_d[l, f, j],
                                           lambda s_: s_[:, :].rearrange("p (a k n) -> p a k n", a=2, k=8)))
            for j in range(j0, j1):
                buf, tok = pend.pop(0)
                if j + 2 < j1:
                    pend.append(self.ring_push("rA", self.wgu_d[l, f, j + 2],
                                               lambda s_: s_[:, :].rearrange("p (a k n) -> p a k n", a=2, k=8)))
                self.bg_step(1)
                jj = j - j0
                for t in range(NT):
                    tsl = slice(t * TS, (t + 1) * TS)
                    bG = t % 2
                    bU = 2 + t % 2
                    for k in range(8):
                        self.mm(self.ps[bG][:, :], buf[:, 0, k, :], self.hT[:, k, tsl], start=(k == 0), stop=(k == 7),
                                r=[tok, ("h", k, t)], w=[("ps", bG)] if k == 0 else (), wj=[("ps", bG)] if k else ())
                    for k in range(8):
                        self.mm(self.ps[bU][:, :], buf[:, 1, k, :], self.hT[:, k, tsl], start=(k == 0), stop=(k == 7),
                                r=[tok, ("h", k, t)], w=[("ps", bU)] if k == 0 else (), wj=[("ps", bU)] if k else ())
                    b = nsg % 2
                    nsg += 1
                    self.act(sg[:, b, :], self.ps[bG][:, :], AF.Silu, r=[("ps", bG)], w=[("sg", b)])
                    self.tt("dve", AT[:, jj, tsl], sg[:, b, :], self.ps[bU][:, :], ALU.mult,
                            r=[("sg", b), ("ps", bU)], w=[("AT", jj, t)])
                self.ring_free(tok)
            ng_ = j1 - j0
            for o in range(8):
                for t in range(NT):
                    tsl = slice(t * TS, (t + 1) * TS)
                    bD = 4 + nd % 2
                    nd += 1
                    for jj in range(ng_):
                        wdb, wtok = wd[jj]
                        self.mm(self.ps[bD][:, :], wdb[:, o * 128:(o + 1) * 128], AT[:, jj, tsl], start=(jj == 0),
                                stop=(jj == ng_ - 1), r=[wtok, ("AT", jj, t)],
                                w=[("ps", bD)] if jj == 0 else (), wj=[("ps", bD)] if jj else ())
                    self.stt(self.xT[:, o, tsl], self.ps[bD][:, :], self.gcol[:, l, s, o:o + 1], self.xT[:, o, tsl],
                             ALU.mult, ALU.add, r=[("ps", bD), ("gcol", l, s), ("x", o, t)], w=[("x", o, t)])
            for (_, wtok) in wd:
                self.ring_free(wtok)
        self.scr.release(m)
        self.P.barrier()

    def stream(self, ring, srcs, view, depth=2):
        pend = []
        n = len(srcs)
        for i in range(min(depth, n)):
            pend.append(self.ring_push(ring, srcs[i], view))
        for i in range(n):
            buf, tok = pend.pop(0)
            if i + depth < n:
                pend.append(self.ring_push(ring, srcs[i + depth], view))
            yield i, buf, tok
            self.ring_free(tok)

    @staticmethod
    def chunk_view(s_):
        return s_[:, 0:1024].rearrange("p (k n) -> p k n", k=8)

    def proj_fm(self, l, idxs, evac):
        srcs = [self.win_d[l, i] for i in idxs]
        for i, buf, tok in self.stream("rA", srcs, self.chunk_view):
            for t in range(NT):
                tsl = slice(t * TS, (t + 1) * TS)
                b = self.pj_n % 4
                self.pj_n += 1
                for k in range(8):
                    self.mm(self.ps[b][:, :], buf[:, k, :], self.hT[:, k, tsl], start=(k == 0), stop=(k == 7),
                            r=[tok, ("h", k, t)], w=[("ps", b)] if k == 0 else (), wj=[("ps", b)] if k else ())
                evac(i, t, self.ps[b], ("ps", b))

    def evac_copy(self, out, in_, r, w=(), wj=()):
        eng = "act" if self.ev_n % 2 == 0 else "dve"
        self.ev_n += 1
        self.copy(eng, out, in_, r=r, w=w, wj=wj)

    def mixer(self, l):
        self.pj_n = 0
        self.ev_n = 0
        self.norm(l, 1)
        m0 = self.scr.mark()
        self.oT_dil = self.scr.alloc([1, S], BF16)
        self.dil_phase(l)
        self.P.barrier()
        if self.stop == f"odil{l}":
            self.stop = "__done__"
            return self.dump_o([(self.oT_dil, 0, "odil", 2)])
        self.oT_sb = self.scr.alloc([2, S], BF16)
        self.sb_phase(l)
        self.P.barrier()
        if self.stop == f"osb{l}":
            self.stop = "__done__"
            return self.dump_o([(self.oT_dil, 0, "odil", 2), (self.oT_sb, 0, "osb", 0), (self.oT_sb, 1, "osb", 1)])
        self.oT_swa = self.scr.alloc([3, S], BF16)
        self.swa_phase(l)
        self.P.barrier()
        if self.stop == f"o{l}":
            self.stop = "__done__"
            return self.dump_o([(self.oT_sb, 0, "osb", 0), (self.oT_sb, 1, "osb", 1), (self.oT_dil, 0, "odil", 2),
                                (self.oT_swa, 0, "oswa", 3), (self.oT_swa, 1, "oswa", 4), (self.oT_swa, 2, "oswa", 5)])
        self.merge_phase(l)
        self.scr.release(m0)
        self.P.barrier()

    def dump_o(self, srcs):
        for (buf, c, nm, i) in srcs:
            for t in range(NT):
                tsl = slice(t * TS, (t + 1) * TS)
                self.copy("dve", self.xT[:, i, tsl], buf[:, c, tsl], r=[(nm, c, t)], w=[("x", i, t)])
        return self.dump_x()

    def banded_head(self, units, E, fin):
        Pf, Pm = self.bd_Pf, self.bd_Pm
        started = set()
        lastpv = {}
        for ui, u in enumerate(units):
            for pi, pv in enumerate(u["pv"]):
                lastpv[pv[0]] = (ui, pi)
        n = len(units)

        def stage_a(ui):
            u = units[ui]
            g = self.bd_n + ui
            b = g % 2
            nk = len(u["S"])
            for si, (lhsT, rhs, toks) in enumerate(u["S"]):
                self.mm(self.ps[b][:, si * 128:(si + 1) * 128], lhsT, rhs, start=True, stop=True, r=toks,
                        w=[("ps", b)] if si == 0 else (), wj=[("ps", b)] if si else ())
            w_ = nk * 128
            self.act(Pf[:, g % 2, 0:w_], self.ps[b][:, 0:w_], AF.Exp, r=[("ps", b)], w=[("Pf", g % 2)], scale=0.125)
            self.tt("dve", Pm[:, g % 3, 0:w_], Pf[:, g % 2, 0:w_], E[:, 0:w_], ALU.mult,
                    r=[("Pf", g % 2), "Ebd"], w=[("Pm", g % 3)])

        def stage_b(ui):
            u = units[ui]
            g = self.bd_n + ui
            for pi, (T, ocols, lhsT, rcols, vtok) in enumerate(u["pv"]):
                bank = 2 + T
                first = T not in started
                started.add(T)
                last = lastpv[T] == (ui, pi)
                self.mm(self.ps[bank][0:65, ocols], lhsT, Pm[:, g % 3, rcols], start=first, stop=last,
                        r=[("Pm", g % 3), vtok], w=[("ps", bank)] if first else (), wj=() if first else [("ps", bank)])

        for step in range(n + 1):
            if step < n:
                stage_a(step)
            if step >= 1:
                stage_b(step - 1)
        self.bd_n += n
        for T in range(NT):
            fin(T, 2 + T)

    def banded_fin(self, T, bank, dest, dtok, esink_col, odd):
        den, bc, otmp = self.bd_den, self.bd_bc, self.bd_otmp
        acc = self.ps[bank]
        if esink_col is not None:
            self.ts("dve", den[64:65, :], acc[64:65, :], esink_col, None, ALU.add, None,
                    r=[("ps", bank), "esink"], w=["den"])
            self.recip(den[64:65, :], den[64:65, :], r=["den"], w=["den"])
        else:
            self.recip(den[64:65, :], acc[64:65, :], r=[("ps", bank)], w=["den"])
        self.mm(self.ps[6][0:64, :], self.onesf[64:65, 0:64], den[64:65, :], start=True, stop=True,
                r=["den", "onesf"], w=[("ps", 6)])
        self.copy("act", bc[0:64, :], self.ps[6][0:64, :], r=[("ps", 6)], w=["bc"])
        if not odd:
            self.tt("dve", dest, acc[0:64, :], bc[0:64, :], ALU.mult, r=[("ps", bank), "bc"], wj=[dtok])
        else:
            par = self.bd_on % 2
            self.bd_on += 1
            self.tt("dve", otmp[0:64, par, :], acc[0:64, :], bc[0:64, :], ALU.mult, r=[("ps", bank), "bc"],
                    w=[("otmp", par)])
            self.dma("sp", dest, otmp[0:64, par, :], key=("otmp", par), r=[("otmp", par)], wj=[dtok])

    def banded_alloc(self):
        self.bd_Pf = self.scr.alloc([2, 256], F32)
        self.bd_Pm = self.scr.alloc([3, 256], BF16)
        self.bd_den = self.scr.alloc([512], F32)
        self.bd_bc = self.scr.alloc([512], F32)
        self.bd_otmp = self.scr.alloc([2, 512], BF16)
        self.bd_E = self.scr.alloc([6, 256], F32)
        self.bd_cm = self.scr.alloc([2, 256], F32)
        self.bd_n = 0
        self.bd_on = 0

    def build_E(self, h0, mtype):
        E, cm = self.bd_E, self.bd_cm
        self.dma("sp", E[:, :, :], self.biasT_d[:, h0:h0 + 6, :], key="biasld", w=["Ebd"])
        self.dma("sp", cm[:, :, :], self.cmask_d[:, :, :], key="cmld", w=["cmbd"])
        self.act(E[:, :, :], E[:, :, :], AF.Exp, r=["Ebd"], w=["Ebd"])
        for h in range(6):
            self.tt("dve", E[:, h, :], E[:, h, :], cm[:, mtype, :], ALU.mult, r=["Ebd", "cmbd"], w=["Ebd"])

    def dil_phase(self, l):
        m = self.scr.mark()
        QT = self.scr.alloc([3, S], BF16)
        KT = self.scr.alloc([3, S], BF16)
        Vd = self.scr.alloc([3, 16, 2, 65], BF16)
        self.banded_alloc()
        self.build_E(0, 0)
        for g in range(3):
            self.memset("dve", Vd[:, g, :, :, 64:65], 1.0, wj=[("Vd", g)])

        def evac(i, t, ps, pstok):
            dstT = QT if i < 3 else KT
            g = i % 3
            nm = "QTd" if i < 3 else "KTd"
            if g == 0:
                self.evac_copy(dstT[:, 0, t * TS:(t + 1) * TS], ps[:, :], r=[pstok], wj=[(nm, 0)])
            elif g == 1:
                dst = dstT[:, 1, :].rearrange("p (r m) -> p r m", r=4)[:, :, 128 * t:128 * (t + 1)]
                src = ps[:, :].rearrange("p (i r) -> p r i", r=4)
                self.evac_copy(dst, src, r=[pstok], wj=[(nm, 1)])
            else:
                dst = dstT[:, 2, :].rearrange("p (r m) -> p r m", r=16)[:, :, 32 * t:32 * (t + 1)]
                src = ps[:, :].rearrange("p (i r) -> p r i", r=16)
                self.evac_copy(dst, src, r=[pstok], wj=[(nm, 2)])

        self.proj_fm(l, [6, 7, 8, 9, 10, 11], evac)
        srcs = [self.wv_d[l][:, :, 256 + g * 128:256 + (g + 1) * 128] for g in range(3)]
        for g, buf, tok in self.stream("rA", srcs, self.chunk_view):
            for b4 in range(4):
                bank = self.pj_n % 4
                self.pj_n += 1
                for bi in range(4):
                    blk = b4 * 4 + bi
                    for k in range(8):
                        if g == 0:
                            hsel = self.hT[:, k, blk * 128:(blk + 1) * 128]
                            htoks = [("h", k, blk // 4)]
                        elif g == 1:
                            r_, mb = blk // 4, blk % 4
                            hsel = self.hT[:, k, 512 * mb + r_:512 * (mb + 1):4]
                            htoks = [("h", k, mb)]
                        else:
                            hsel = self.hT[:, k, blk::16]
                            htoks = [("h", k, t) for t in range(NT)]
                        first = (bi == 0 and k == 0)
                        self.mm(self.ps[bank][:, bi * 128:(bi + 1) * 128], hsel, buf[:, k, :], start=(k == 0),
                                stop=(k == 7), r=[tok] + htoks, w=[("ps", bank)] if first else (),
                                wj=() if first else [("ps", bank)])
                src = self.ps[bank][:, :].rearrange("p (b s d) -> p b s d", b=4, s=2)
                self.evac_copy(Vd[:, g, b4 * 4:(b4 + 1) * 4, :, 0:64], src, r=[("ps", bank)], wj=[("Vd", g)])

        for s_ in range(2):
            pb = s_ * 64
            units = []
            for qb in range(16):
                Sl = [(KT[pb:pb + 64, 0, qb * 128:(qb + 1) * 128], QT[pb:pb + 64, 0, qb * 128:(qb + 1) * 128],
                       [("KTd", 0), ("QTd", 0)])]
                pv = [(qb // 4, slice((qb % 4) * 128, (qb % 4 + 1) * 128), Vd[:, 0, qb, s_, :], slice(0, 128), ("Vd", 0))]
                if qb > 0:
                    Sl.append((KT[pb:pb + 64, 0, (qb - 1) * 128:qb * 128], QT[pb:pb + 64, 0, qb * 128:(qb + 1) * 128],
                               [("KTd", 0), ("QTd", 0)]))
                    pv.append((qb // 4, slice((qb % 4) * 128, (qb % 4 + 1) * 128), Vd[:, 0, qb - 1, s_, :],
                               slice(128, 256), ("Vd", 0)))
                units.append(dict(S=Sl, E=self.bd_E[:, 0 + s_, :], pv=pv))
            for r_ in range(4):
                for mb in range(4):
                    q0 = r_ * 512 + mb * 128
                    Sl = [(KT[pb:pb + 64, 1, q0:q0 + 128], QT[pb:pb + 64, 1, q0:q0 + 128], [("KTd", 1), ("QTd", 1)])]
                    oc = slice(r_, 512, 4)
                    pv = [(mb, oc, Vd[:, 1, r_ * 4 + mb, s_, :], slice(0, 128), ("Vd", 1))]
                    if mb > 0:
                        Sl.append((KT[pb:pb + 64, 1, q0 - 128:q0], QT[pb:pb + 64, 1, q0:q0 + 128],
                                   [("KTd", 1), ("QTd", 1)]))
                        pv.append((mb, oc, Vd[:, 1, r_ * 4 + mb - 1, s_, :], slice(128, 256), ("Vd", 1)))
                    units.append(dict(S=Sl, E=self.bd_E[:, 2 + s_, :], pv=pv))
            for r_ in range(16):
                q0 = r_ * 128
                Sl = [(KT[pb:pb + 64, 2, q0:q0 + 128], QT[pb:pb + 64, 2, q0:q0 + 128], [("KTd", 2), ("QTd", 2)])]
                pv = [(T, slice(r_, 512, 16), Vd[:, 2, r_, s_, :], slice(32 * T, 32 * (T + 1)), ("Vd", 2))
                      for T in range(NT)]
                units.append(dict(S=Sl, E=self.bd_E[:, 4 + s_, :], pv=pv))

            def fin(T, bank, s_=s_):
                tsl = slice(T * TS, (T + 1) * TS)
                self.banded_fin(T, bank, self.oT_dil[pb:pb + 64, 0, tsl], ("odil", 0, T), None, odd=(s_ == 1))

            self.banded_units(units, fin)
        self.scr.release(m)

    def banded_units(self, units, fin):
        Pf, Pm = self.bd_Pf, self.bd_Pm
        started = set()
        lastpv = {}
        for ui, u in enumerate(units):
            for pi, pv in enumerate(u["pv"]):
                lastpv[pv[0]] = (ui, pi)
        n = len(units)

        def stage_a(ui):
            u = units[ui]
            g = self.bd_n + ui
            b = g % 2
            nk = len(u["S"])
            for si, (lhsT, rhs, toks) in enumerate(u["S"]):
                self.mm(self.ps[b][:, si * 128:(si + 1) * 128], lhsT, rhs, start=True, stop=True, r=toks,
                        w=[("ps", b)] if si == 0 else (), wj=[("ps", b)] if si else ())
            w_ = nk * 128
            self.act(Pf[:, g % 2, 0:w_], self.ps[b][:, 0:w_], AF.Exp, r=[("ps", b)], w=[("Pf", g % 2)], scale=0.125)
            self.tt("dve", Pm[:, g % 3, 0:w_], Pf[:, g % 2, 0:w_], u["E"][:, 0:w_], ALU.mult,
                    r=[("Pf", g % 2), "Ebd"], w=[("Pm", g % 3)])

        def stage_b(ui):
            u = units[ui]
            g = self.bd_n + ui
            for pi, (T, ocols, lhsT, rcols, vtok) in enumerate(u["pv"]):
                bank = 2 + T
                first = T not in started
                started.add(T)
                last = lastpv[T] == (ui, pi)
                self.mm(self.ps[bank][0:65, ocols], lhsT, Pm[:, g % 3, rcols], start=first, stop=last,
                        r=[("Pm", g % 3), vtok], w=[("ps", bank)] if first else (), wj=() if first else [("ps", bank)])

        for step in range(n + 1):
            if step < n:
                stage_a(step)
            if step >= 1:
                stage_b(step - 1)
        self.bd_n += n
        for T in range(NT):
            fin(T, 2 + T)

    def swa_phase(self, l):
        m = self.scr.mark()
        QT = self.scr.alloc([3, S], BF16)
        KAB = self.scr.alloc([2, S], BF16)
        Vs = self.scr.alloc([16, 2, 65], BF16)
        self.banded_alloc()
        self.build_E(6, 1)
        self.memset("dve", Vs[:, :, :, 64:65], 1.0, wj=["Vs"])

        def evac(i, t, ps, pstok):
            tsl = slice(t * TS, (t + 1) * TS)
            if i < 3:
                self.evac_copy(QT[:, i, tsl], ps[:, :], r=[pstok], wj=[("QTw", i)])
            else:
                self.evac_copy(KAB[:, i - 3, tsl], ps[:, :], r=[pstok], wj=[("KAB", i - 3)])

        self.proj_fm(l, [15, 16, 17, 18, 44], evac)
        srcs = [self.wv_d[l][:, :, 640:768]]
        for g, buf, tok in self.stream("rA", srcs, self.chunk_view):
            for b4 in range(4):
                bank = self.pj_n % 4
                self.pj_n += 1
                for bi in range(4):
                    blk = b4 * 4 + bi
                    for k in range(8):
                        first = (bi == 0 and k == 0)
                        self.mm(self.ps[bank][:, bi * 128:(bi + 1) * 128], self.hT[:, k, blk * 128:(blk + 1) * 128],
                                buf[:, k, :], start=(k == 0), stop=(k == 7), r=[tok, ("h", k, blk // 4)],
                                w=[("ps", bank)] if first else (), wj=() if first else [("ps", bank)])
                src = self.ps[bank][:, :].rearrange("p (b s d) -> p b s d", b=4, s=2)
                self.evac_copy(Vs[:, b4 * 4:(b4 + 1) * 4, :, 0:64], src, r=[("ps", bank)], wj=["Vs"])

        for h in range(6):
            c, pb, kv = h // 2, (h % 2) * 64, h // 3
            if kv == 0:
                ksel = 0 if pb == 0 else 1
            else:
                ksel = 1 if pb == 0 else 0
            units = []
            for qb in range(16):
                Sl = [(KAB[pb:pb + 64, ksel, qb * 128:(qb + 1) * 128], QT[pb:pb + 64, c, qb * 128:(qb + 1) * 128],
                       [("KAB", ksel), ("QTw", c)])]
                oc = slice((qb % 4) * 128, (qb % 4 + 1) * 128)
                pv = [(qb // 4, oc, Vs[:, qb, kv, :], slice(0, 128), "Vs")]
                if qb > 0:
                    Sl.append((KAB[pb:pb + 64, ksel, (qb - 1) * 128:qb * 128],
                               QT[pb:pb + 64, c, qb * 128:(qb + 1) * 128], [("KAB", ksel), ("QTw", c)]))
                    pv.append((qb // 4, oc, Vs[:, qb - 1, kv, :], slice(128, 256), "Vs"))
                units.append(dict(S=Sl, E=self.bd_E[:, h, :], pv=pv))

            def fin(T, bank, h=h, c=c, pb=pb):
                tsl = slice(T * TS, (T + 1) * TS)
                self.banded_fin(T, bank, self.oT_swa[pb:pb + 64, c, tsl], ("oswa", c, T),
                                self.esink[64:65, l, h:h + 1], odd=(pb == 64))

            self.banded_units(units, fin)
        self.scr.release(m)

    def sb_phase(self, l):
        m = self.scr.mark()
        QT = self.scr.alloc([S], BF16)
        KTh = self.scr.alloc([2, S], BF16)
        KnTh = self.scr.alloc([2, S], BF16)
        Vsb = self.scr.alloc([16, 256], BF16)
        E1 = self.scr.alloc([2, 512], F32)
        Lb = self.scr.alloc([4, 512], BF16)
        Ls = self.scr.alloc([4, 512], BF16)
        Wb = self.scr.alloc([3, 512], BF16)
        otmp = self.scr.alloc([2, 512], BF16)
        ONES, TRI, IDN, IDN8 = (self.cb16[:, i, :] for i in range(4))
        skip = os.environ.get("SB_SKIP", "")
        for hh_ in range(2):
            if "m" in skip:
                break
            for t_ in range(NT):
                self.memset("dve", KTh[:, hh_, t_ * TS:(t_ + 1) * TS], 0.0, wj=["KThz"])
                self.memset("dve", KnTh[:, hh_, t_ * TS:(t_ + 1) * TS], 0.0, wj=["KnThz"])

        srcs = [self.wv_d[l][:, :, 0:256]] if "v" not in skip else []
        for g, buf, tok in self.stream("rA", srcs, lambda s_: s_[:, :].rearrange("p (k n) -> p k n", k=8)):
            for b2 in range(8):
                bank = self.pj_n % 4
                self.pj_n += 1
                for bi in range(2):
                    blk = b2 * 2 + bi
                    for k in range(8):
                        first = (bi == 0 and k == 0)
                        self.mm(self.ps[bank][:, bi * 256:(bi + 1) * 256], self.hT[:, k, blk * 128:(blk + 1) * 128],
                                buf[:, k, :], start=(k == 0), stop=(k == 7), r=[tok, ("h", k, blk // 4)],
                                w=[("ps", bank)] if first else (), wj=() if first else [("ps", bank)])
                src = self.ps[bank][:, :].rearrange("p (b n) -> p b n", b=2)
                self.evac_copy(Vsb[:, b2 * 2:(b2 + 1) * 2, :], src, r=[("ps", bank)], wj=[("Vsb", b2 // 2)])

        self.sb_gi = 0
        self.sb_s = 0
        for c in range(2):
            def evac(i, t, ps, pstok, c=c):
                tsl = slice(t * TS, (t + 1) * TS)
                ev = os.environ.get("SB_EV", "qabcd")
                if i == 0:
                    if "q" in ev:
                        self.evac_copy(QT[:, tsl], ps[:, :], r=[pstok], w=[("QTs", t)])
                else:
                    if "a" in ev:
                        self.copy("act", KTh[0:64, 0, tsl], ps[0:64, :], r=[pstok, "KThz"], w=[("KTh", 0, t)])
                    if "b" in ev:
                        self.copy("act", KTh[64:128, 1, tsl], ps[64:128, :], r=[pstok, "KThz"], w=[("KTh", 1, t)])
                    if "c" in ev:
                        self.ts("dve", KnTh[0:64, 0, tsl], KTh[0:64, 0, tsl], -0.125, None, ALU.mult, None,
                                r=[("KTh", 0, t), "KnThz"], w=[("KnTh", 0, t)])
                    if "d" in ev:
                        self.ts("dve", KnTh[64:128, 1, tsl], KTh[64:128, 1, tsl], -0.125, None, ALU.mult, None,
                                r=[("KTh", 1, t), "KnThz"], w=[("KnTh", 1, t)])

            if "p" not in skip:
                self.proj_fm(l, [c, 2 + c], evac)
            self.sb_chunk(l, c, QT, KTh, KnTh, Vsb, E1, Lb, Ls, Wb, otmp)
        self.scr.release(m)

    def sb_chunk(self, l, c, QT, KTh, KnTh, Vsb, E1, Lb, Ls, Wb, otmp):
        ONES, TRI, IDN, IDN8 = (self.cb16[:, i, :] for i in range(4))
        its = []
        for hh in range(2):
            for T in range(NT):
                kbs = list(range(4 * T + 3, -1, -1))
                for idx, kb in enumerate(kbs):
                    its.append(dict(hh=hh, T=T, kb=kb, first=(idx == 0), last=(kb == 0), g=self.sb_gi))
                self.sb_gi += 1
        state = {"a": {}}
        s0 = self.sb_s

        def geom(it):
            hh, T, kb = it["hh"], it["T"], it["kb"]
            dk = kb - 4 * T
            c0 = 128 * dk if dk > 0 else 0
            return hh, T, kb, dk, c0

        def stage_a(i):
            it = its[i]
            s = s0 + i
            hh, T, kb, dk, c0 = geom(it)
            zb = s % 2
            g2 = it["g"] % 2
            if it["first"]:
                self.bg_step(3)
                self.memset("dve", Ls[:, 2 * g2, :], 0.0, w=[("Ls", g2, 0)])
                self.memset("dve", Ls[:, 2 * g2 + 1, :], 0.0, w=[("Ls", g2, 1)])
                state["a"][it["g"]] = 0
            qsl = slice(T * TS + c0, (T + 1) * TS)
            self.mm(self.ps[zb][:, c0:512], KTh[:, hh, kb * 128:(kb + 1) * 128], QT[:, qsl],
                    start=True, stop=(dk < 0), r=[("KTh", hh, kb // 4), ("QTs", T)], w=[("ps", zb)])
            if dk >= 0:
                self.mm(self.ps[zb][:, c0:512], IDN8, self.MB[:, dk, c0:512], start=False, stop=True,
                        r=["cb16", "MB"], wj=[("ps", zb)])
            self.act(E1[:, s % 2, c0:512], self.ps[zb][:, c0:512], AF.Exp, r=[("ps", zb)], w=[("E1", s % 2)],
                     scale=0.125)
            self.act(Lb[:, s % 4, c0:512], E1[:, s % 2, c0:512], AF.Ln, r=[("E1", s % 2)], w=[("L", s % 4)],
                     bias=1.0)

        def stage_b(i):
            it = its[i]
            s = s0 + i
            hh, T, kb, dk, c0 = geom(it)
            cb = 2 + s % 2
            g2 = it["g"] % 2
            a = state["a"][it["g"]]
            qsl = slice(T * TS + c0, (T + 1) * TS)
            self.mm(self.ps[cb][:, c0:512], TRI, Lb[:, s % 4, c0:512], start=True, stop=False,
                    r=[("L", s % 4), "cb16"], w=[("ps", cb)])
            self.mm(self.ps[cb][:, c0:512], ONES, Ls[:, 2 * g2 + a, c0:512], start=False, stop=False,
                    r=[("Ls", g2, a), "cb16"], wj=[("ps", cb)])
            self.mm(self.ps[cb][:, c0:512], KnTh[:, hh, kb * 128:(kb + 1) * 128], QT[:, qsl],
                    start=False, stop=(dk < 0), r=[("KnTh", hh, kb // 4), ("QTs", T)], wj=[("ps", cb)])
            if dk >= 0:
                self.mm(self.ps[cb][:, c0:512], IDN, self.MB[:, dk, c0:512], start=False, stop=True,
                        r=["cb16", "MB"], wj=[("ps", cb)])
            self.act(Wb[:, s % 3, c0:512], self.ps[cb][:, c0:512], AF.Exp, r=[("ps", cb)], w=[("W", s % 3)],
                     scale=-1.0)
            if not it["last"]:
                self.tt("dve", Ls[:, 2 * g2 + (1 - a), c0:512], Ls[:, 2 * g2 + a, c0:512], Lb[:, s % 4, c0:512],
                        ALU.add, r=[("Ls", g2, a), ("L", s % 4)], w=[("Ls", g2, 1 - a)])
                state["a"][it["g"]] = 1 - a

        def stage_c(i):
            it = its[i]
            s = s0 + i
            hh, T, kb, dk, c0 = geom(it)
            h = 2 * c + hh
            ob = 4 + it["g"] % 2
            self.mm(self.ps[ob][:, c0:512], Vsb[:, kb, c * 128:(c + 1) * 128], Wb[:, s % 3, c0:512],
                    start=it["first"], stop=it["last"], r=[("W", s % 3), ("Vsb", kb // 4)],
                    w=[("ps", ob)] if it["first"] else (), wj=() if it["first"] else [("ps", ob)])
            if it["last"]:
                tsl = slice(T * TS, (T + 1) * TS)
                pb = hh * 64
                self.copy("dve", self.oT_sb[pb:pb + 64, c, tsl], self.ps[ob][pb:pb + 64, :], r=[("ps", ob)],
                          wj=[("osb", c, T)])

        n = len(its)
        import os
        lim = int(os.environ.get("SB_LIMIT", "0"))
        if lim:
            n = min(n, lim)
        stg = int(os.environ.get("SB_STAGE", "3"))
        if stg == 0:
            n = 0
        for step in range(n + 2):
            if step < n:
                stage_a(step)
            if 1 <= step <= n and stg >= 2:
                stage_b(step - 1)
            if step >= 2 and stg >= 3:
                stage_c(step - 2)
        self.sb_s += n

    def merge_phase(self, l):
        m = self.scr.mark()
        MT = self.scr.alloc([8, S], BF16)
        accb = self.scr.alloc([4, 512], F32)
        sgm = self.scr.alloc([512], F32)
        prod = self.scr.alloc([512], F32)
        wbr = [self.ring_push("rD", self.wbr_d[l, i], lambda s_: s_[:, :]) for i in range(6)]
        branches = [
            ("sb", 20, [(self.oT_sb, 0, "osb", wbr[0]), (self.oT_sb, 1, "osb", wbr[1])]),
            ("dil", 28, [(self.oT_dil, 0, "odil", wbr[2])]),
            ("swa", 36, [(self.oT_swa, 0, "oswa", wbr[3]), (self.oT_swa, 1, "oswa", wbr[4]),
                         (self.oT_swa, 2, "oswa", wbr[5])]),
        ]
        idxs = []
        for o in range(8):
            for (_, base, _) in branches:
                idxs.append(base + o)
        srcs = [self.win_d[l, i] for i in idxs]
        n = 0
        for i, buf, tok in self.stream("rA", srcs, self.chunk_view):
            o, bi = i // 3, i % 3
            ks = branches[bi][2]
            for t in range(NT):
                tsl = slice(t * TS, (t + 1) * TS)
                bG = n % 2
                bY = 2 + n % 2
                n += 1
                for k in range(8):
                    self.mm(self.ps[bG][:, :], buf[:, k, :], self.hT[:, k, tsl], start=(k == 0), stop=(k == 7),
                            r=[tok, ("h", k, t)], w=[("ps", bG)] if k == 0 else (), wj=[("ps", bG)] if k else ())
                for ki, (ob, oc, onm, (wb_, wtok)) in enumerate(ks):
                    self.mm(self.ps[bY][:, :], wb_[:, o * 128:(o + 1) * 128], ob[:, oc, tsl], start=(ki == 0),
                            stop=(ki == len(ks) - 1), r=[wtok, (onm, oc, t)],
                            w=[("ps", bY)] if ki == 0 else (), wj=[("ps", bY)] if ki else ())
                self.act(sgm[:, :], self.ps[bG][:, :], AF.Sigmoid, r=[("ps", bG)], w=["sgm"])
                if bi == 0:
                    self.tt("dve", accb[:, t, :], sgm[:, :], self.ps[bY][:, :], ALU.mult, r=["sgm", ("ps", bY)],
                            w=[("accb", t)])
                else:
                    self.tt("dve", prod[:, :], sgm[:, :], self.ps[bY][:, :], ALU.mult, r=["sgm", ("ps", bY)],
                            w=["prod"])
                    if bi == 1:
                        self.tt("dve", accb[:, t, :], accb[:, t, :], prod[:, :], ALU.add, r=[("accb", t), "prod"],
                                w=[("accb", t)])
                    else:
                        self.tt("dve", MT[:, o, tsl], accb[:, t, :], prod[:, :], ALU.add, r=[("accb", t), "prod"],
                                w=[("MT", o, t)])
        for (_, wtok) in wbr:
            self.ring_free(wtok)
        if self.stop == f"merged{l}":
            self.stop = "__done__"
            for o in range(8):
                for t in range(NT):
                    tsl = slice(t * TS, (t + 1) * TS)
                    self.copy("dve", self.xT[:, o, tsl], MT[:, o, tsl], r=[("MT", o, t)], w=[("x", o, t)])
            self.dump_x()
            self.scr.release(m)
            return
        srcs = [self.wout_d[l, o] for o in range(8)]
        nd = 0
        for o, buf, tok in self.stream("rA", srcs, self.chunk_view):
            for t in range(NT):
                tsl = slice(t * TS, (t + 1) * TS)
                bD = 4 + nd % 2
                nd += 1
                for k in range(8):
                    self.mm(self.ps[bD][:, :], buf[:, k, :], MT[:, k, tsl], start=(k == 0), stop=(k == 7),
                            r=[tok, ("MT", k, t)], w=[("ps", bD)] if k == 0 else (), wj=[("ps", bD)] if k else ())
                self.stt(self.xT[:, o, tsl], self.ps[bD][:, :], self.gcol[:, l, 1, o:o + 1], self.xT[:, o, tsl],
                         ALU.mult, ALU.add, r=[("ps", bD), ("gcol", l, 1), ("x", o, t)], w=[("x", o, t)])
        self.scr.release(m)

    def final_norm(self):
        m = self.scr.mark()
        sq = self.scr.alloc([3, 512], BF16)
        sd = self.scr.alloc([2, 512], F32)
        rstd = self.scr.alloc([2, 512], F32)
        ONES = self.cb16[:, 0, :]
        psN = self.ps[6]
        for t in range(NT):
            tsl = slice(t * TS, (t + 1) * TS)
            for c in range(8):
                self.act(sq[:, c % 3, :], self.xT[:, c, tsl], AF.Square, r=[("x", c, t)], w=[("sq", c % 3)])
                self.mm(psN[:, :], ONES, sq[:, c % 3, :], start=(c == 0), stop=(c == 7),
                        r=[("sq", c % 3), "cb16"], w=[("ps", 6)] if c == 0 else (), wj=[("ps", 6)] if c > 0 else ())
            self.act(sd[:, t % 2, :], psN[:, :], AF.Sqrt, r=[("ps", 6)], w=[("sd", t % 2)], bias=EPS,
                     scale=1.0 / D)
            self.recip(rstd[:, t % 2, :], sd[:, t % 2, :], r=[("sd", t % 2)], w=[("rstd", t % 2)])
            for c in range(8):
                self.stt(self.xT[:, c, tsl], self.xT[:, c, tsl], self.fg[:, c:c + 1], rstd[:, t % 2, :],
                         ALU.mult, ALU.mult, r=[("x", c, t), "fg", ("rstd", t % 2)], w=[("x", c, t)])
                self.dma("sp", self.outT_d[c * 128:(c + 1) * 128, tsl], self.xT[:, c, tsl], key="out",
                         r=[("x", c, t)])
        self.P.final_dma.append("out")
        self.scr.release(m)


def t5_bucket_np(n):
    n = np.asarray(n, np.int64)
    max_exact = 16
    nf = np.maximum(n, 1).astype(np.float32)
    large = max_exact + (np.log(nf / np.float32(max_exact)) / np.float32(np.log(2048 / 16))
                         * np.float32(16)).astype(np.int32)
    large = np.minimum(large, 31)
    return np.where(n < max_exact, n, large)


def prep_shared(inp):
    f = np.float32
    c_ = np.ascontiguousarray
    w_ada = inp["w_ada"]
    sh = {}
    sh["wada_t"] = c_(w_ada.reshape(L, 8, 128, 72, 128).transpose(0, 3, 2, 1, 4))
    sh["bada_c"] = c_(inp["b_ada"].reshape(L, 72, 128).transpose(2, 0, 1))
    sh["ng_c"] = c_(inp["norm_gain"].reshape(L, 3, 8, 128).transpose(3, 0, 1, 2))
    sh["fg_c"] = c_(inp["final_gain"].reshape(8, 128).T)
    g = inp["w_ffn_gate"].reshape(L, 2, 8, 128, NJ, 128).transpose(0, 1, 4, 3, 2, 5)
    u = inp["w_ffn_up"].reshape(L, 2, 8, 128, NJ, 128).transpose(0, 1, 4, 3, 2, 5)
    sh["wgu"] = c_(np.stack([g, u], axis=4))
    sh["wd"] = c_(inp["w_ffn_down"].reshape(L, 2, NJ, 128, D))
    w_in = inp["w_in"]
    wt = w_in.reshape(L, 8, 128, 44, 128).transpose(0, 3, 2, 1, 4)
    ksw = w_in[:, :, 2304:2432]
    ksw_sw = np.concatenate([ksw[:, :, 64:128], ksw[:, :, 0:64]], axis=2)
    ext = ksw_sw.reshape(L, 8, 128, 1, 128).transpose(0, 3, 2, 1, 4)
    sh["win_t"] = c_(np.concatenate([wt, ext], axis=1))
    wv = np.concatenate([w_in[:, :, 512:768], w_in[:, :, 1536:1920], w_in[:, :, 2432:2560]], axis=2)
    sh["wv"] = c_(wv.reshape(L, 8, 128, 768).transpose(0, 2, 1, 3))
    wbr = np.concatenate([inp["w_br_sb"], inp["w_br_dil"], inp["w_br_swa"]], axis=1)
    sh["wbr"] = c_(wbr.reshape(L, 6, 128, D))
    sh["wout_t"] = c_(inp["w_out"].reshape(L, 8, 128, 8, 128).transpose(0, 3, 2, 1, 4))
    sh["sinks_b"] = c_(np.broadcast_to(inp["sinks"][None], (128, L, 6)))
    p = np.arange(128)[:, None]
    fq = np.arange(256)[None, :]
    rel = np.maximum(fq - p, 0)
    bt = np.zeros((128, 12, 256), f)
    for hidx in range(12):
        dil = (1, 4, 16)[hidx // 2] if hidx < 6 else 1
        bt[:, hidx, :] = inp["rel_bias"][t5_bucket_np(rel * dil), hidx]
    sh["biasT"] = bt
    relm = fq - p
    cm = np.zeros((128, 2, 256), f)
    cm[:, 0, :] = ((relm >= 0) & (relm <= 128)).astype(f)
    cm[:, 1, :] = ((relm >= 0) & (relm <= 127)).astype(f)
    sh["cmask"] = cm
    cst = np.zeros((128, 4, 128), f)
    cst[:, 0, :] = 1.0
    j = np.arange(128)[:, None]
    s_ = np.arange(128)[None, :]
    cst[:, 1, :] = (j >= s_).astype(f)
    cst[:, 2, :] = np.eye(128, dtype=f)
    cst[:, 3, :] = -8.0 * np.eye(128, dtype=f)
    sh["cst"] = cst
    mb = np.zeros((128, 4, 512), f)
    for dk in range(4):
        key = 128 * dk + np.arange(128)[:, None]
        q = np.arange(512)[None, :]
        mb[:, dk, :] = np.where(key >= q, BIG, 0.0)
    sh["mbias"] = mb
    return sh


def prep_core(inp, b):
    return {
        "xT": np.ascontiguousarray(inp["x"][b].T),
        "c_c": np.ascontiguousarray(inp["c"][b].reshape(8, 128).T),
    }


_NC_CACHE = {}


def kernel(**inputs):
    inp = {k: np.asarray(v) for k, v in inputs.items()}
    sh = prep_shared(inp)
    if "nc" not in _NC_CACHE:
        _NC_CACHE["nc"] = Builder().build()
    nc = _NC_CACHE["nc"]
    in_maps = []
    for b in range(8):
        m = dict(sh)
        m.update(prep_core(inp, b))
        in_maps.append(m)
    res = run_bass_kernel_spmd(nc, in_maps, core_ids=list(range(8)))
    out = np.stack([np.ascontiguousarray(r["outT"].T) for r in res.results], axis=0)
    return out.astype(np.float32)
```
